# Optimizing a Trainium2 kernel written in Bass

```python
import math
import jax, jax.numpy as jnp
from jax import lax
import numpy as np

D_MODEL = 1024
BATCH = 2
SEQ = 8192
DEPTH = 4

MEM_LEN = 256
D_RNN = D_MODEL
LRU_BLOCKS = 8
LRU_BW = D_RNN // LRU_BLOCKS
CONV_W = 4
LRU_C = 8.0
N_HEADS = 16
HEAD_DIM = 64
N_KV = 4
GROUP = N_HEADS // N_KV
NSA_W = N_HEADS * HEAD_DIM
KV_W = N_KV * HEAD_DIM
CMP_STRIDE = 16
CMP_LEN = 2 * CMP_STRIDE
CMP_HIDDEN = 4 * HEAD_DIM
SEL_LEN = 64
N_SELECT = 16
WINDOW = 512
Q_BLOCK = 128
MEM_HEADS = 4
MEM_HEAD_DIM = D_MODEL // MEM_HEADS
MEM_W = MEM_HEADS * MEM_HEAD_DIM
D_FF = 4 * D_MODEL
ROPE_THETA = 10000.0
EPS = 1e-6
NEG = -1e30
FORCE = 1e4

IN_SPLITS = (D_RNN, D_RNN, NSA_W, 6 * KV_W, 3 * N_HEADS, MEM_W, 3 * D_MODEL)
D_IN = sum(IN_SPLITS)
IN_OFFSETS = tuple(int(v) for v in np.cumsum(IN_SPLITS)[:-1])

kernel_name = "hybrid_rglru_nsa_memory_block"


def rmsnorm(x, g):
    xf = x.astype(jnp.float32)
    y = xf * lax.rsqrt(jnp.mean(xf * xf, axis=-1, keepdims=True) + EPS)
    return (y * g.astype(jnp.float32)).astype(x.dtype)


def rope(x, pos):
    hd = x.shape[-1]
    half = hd // 2
    inv = ROPE_THETA ** (-jnp.arange(half, dtype=jnp.float32) * 2.0 / hd)
    ang = pos.astype(jnp.float32)[..., None] * inv
    cos = jnp.cos(ang)[:, :, None, :]
    sin = jnp.sin(ang)[:, :, None, :]
    xf = x.astype(jnp.float32)
    x1, x2 = xf[..., :half], xf[..., half:]
    return jnp.concatenate([x1 * cos - x2 * sin, x2 * cos + x1 * sin], axis=-1).astype(x.dtype)


def masked_softmax(sc, mask):
    sc = jnp.where(mask, sc.astype(jnp.float32), NEG)
    m = jnp.max(sc, axis=-1, keepdims=True)
    p = jnp.where(mask, jnp.exp(sc - m), 0.0)
    return p / jnp.maximum(jnp.sum(p, axis=-1, keepdims=True), 1e-30)


def block_diag_linear(x, w, b):
    B_, S_, C = x.shape
    xb = x.reshape(B_, S_, LRU_BLOCKS, LRU_BW)
    return (jnp.einsum('bsnc,ncd->bsnd', xb, w) + b).reshape(B_, S_, C)


def rglru_branch(xr, yr, conv_w, conv_b, wr, br, wi, bi, lam):
    xc = lax.conv_general_dilated(
        xr, conv_w[:, None, :], window_strides=(1,), padding=[(CONV_W - 1, 0)],
        dimension_numbers=('NWC', 'WIO', 'NWC'), feature_group_count=D_RNN) + conv_b
    r = jax.nn.sigmoid(block_diag_linear(xc, wr, br))
    i = jax.nn.sigmoid(block_diag_linear(xc, wi, bi))
    log_a = -LRU_C * jax.nn.softplus(-lam.astype(jnp.float32)) * r.astype(jnp.float32)
    a = jnp.exp(log_a)
    b = jnp.sqrt(-jnp.expm1(2.0 * log_a)) * (i * xc).astype(jnp.float32)

    def combine(left, right):
        a1, b1 = left
        a2, b2 = right
        return a1 * a2, a2 * b1 + b2

    _, h = lax.associative_scan(combine, (a, b), axis=1)
    return h.astype(xr.dtype) * jax.nn.gelu(yr)


def nsa_branch(q, kv, gates, positions, cmp_pe, cmp_w1, cmp_b1, cmp_w2):
    B_, S_, _ = q.shape
    n_cmp = S_ // CMP_STRIDE - 1
    n_sel = S_ // SEL_LEN
    top_n = min(N_SELECT, n_sel)
    n_qblk = S_ // Q_BLOCK
    scale = HEAD_DIM ** -0.5

    q = rope(q.reshape(B_, S_, N_HEADS, HEAD_DIM), positions)
    q = q.reshape(B_, S_, N_KV, GROUP, HEAD_DIM).transpose(0, 2, 3, 1, 4)
    kv = kv.reshape(B_, S_, 6, N_KV, HEAD_DIM)
    k_c, v_c, k_s, v_s, k_w, v_w = [kv[:, :, j] for j in range(6)]

    def compress(t, j):
        chunks = t.reshape(B_, S_ // CMP_STRIDE, CMP_STRIDE, N_KV, HEAD_DIM)
        blocks = jnp.concatenate([chunks[:, :-1], chunks[:, 1:]], axis=2)
        blocks = blocks + cmp_pe[j][None, None, :, None, :]
        flat = blocks.transpose(0, 1, 3, 2, 4).reshape(B_, n_cmp, N_KV, CMP_LEN * HEAD_DIM)
        hid = jax.nn.gelu(flat @ cmp_w1[j] + cmp_b1[j])
        return hid @ cmp_w2[j]

    cmp_end = jnp.arange(n_cmp) * CMP_STRIDE + CMP_LEN - 1
    k_cmp = rope(compress(k_c, 0), positions[:, cmp_end]).transpose(0, 2, 1, 3)
    v_cmp = compress(v_c, 1).transpose(0, 2, 1, 3)

    k_sel = rope(k_s, positions).transpose(0, 2, 1, 3).reshape(B_, N_KV, n_sel, SEL_LEN, HEAD_DIM)
    v_sel = v_s.transpose(0, 2, 1, 3).reshape(B_, N_KV, n_sel, SEL_LEN, HEAD_DIM)
    c0 = jnp.arange(n_cmp)[:, None] * CMP_STRIDE
    s0 = jnp.arange(n_sel)[None, :] * SEL_LEN
    overlap = jnp.clip(jnp.minimum(c0 + CMP_LEN, s0 + SEL_LEN) - jnp.maximum(c0, s0), 0, None).astype(jnp.float32) / CMP_LEN

    pad = ((0, 0), (0, 0), (WINDOW, 0), (0, 0))
    k_win = jnp.pad(rope(k_w, positions).transpose(0, 2, 1, 3), pad)
    v_win = jnp.pad(v_w.transpose(0, 2, 1, 3), pad)

    g = jax.nn.sigmoid(gates.astype(jnp.float32)).reshape(B_, S_, N_KV, GROUP, 3).transpose(0, 2, 3, 1, 4)
    bi = jnp.arange(B_)[:, None, None, None]
    gi = jnp.arange(N_KV)[None, :, None, None]
    blk = jnp.arange(n_sel)

    def one_block(i):
        s = i * Q_BLOCK
        t = s + jnp.arange(Q_BLOCK)
        qb = lax.dynamic_slice_in_dim(q, s, Q_BLOCK, axis=3)
        gb = lax.dynamic_slice_in_dim(g, s, Q_BLOCK, axis=3)
        sc = jnp.einsum('bgrqd,bgcd->bgrqc', qb, k_cmp) * scale
        p_c = masked_softmax(sc, cmp_end[None, :] <= t[:, None])
        o_c = jnp.einsum('bgrqc,bgcd->bgrqd', p_c.astype(v_cmp.dtype), v_cmp)
        imp = jnp.einsum('bgrqc,cj->bgqj', p_c, overlap)
        forced = (blk[None, :] == 0) | (blk[None, :] == (t // SEL_LEN)[:, None])
        causal_blk = blk[None, :] * SEL_LEN <= t[:, None]
        imp = jnp.where(forced, FORCE, jnp.where(causal_blk, imp, -FORCE))
        _, idx = lax.top_k(imp, top_n)
        k_g = k_sel[bi, gi, idx].reshape(B_, N_KV, Q_BLOCK, top_n * SEL_LEN, HEAD_DIM)
        v_g = v_sel[bi, gi, idx].reshape(B_, N_KV, Q_BLOCK, top_n * SEL_LEN, HEAD_DIM)
        kpos = (idx[..., None] * SEL_LEN + jnp.arange(SEL_LEN)).reshape(B_, N_KV, Q_BLOCK, top_n * SEL_LEN)
        sc = jnp.einsum('bgrqd,bgqkd->bgrqk', qb, k_g) * scale
        p_s = masked_softmax(sc, (kpos <= t[:, None])[:, :, None])
        o_s = jnp.einsum('bgrqk,bgqkd->bgrqd', p_s.astype(v_g.dtype), v_g)
        k_wb = lax.dynamic_slice_in_dim(k_win, s, Q_BLOCK + WINDOW, axis=2)
        v_wb = lax.dynamic_slice_in_dim(v_win, s, Q_BLOCK + WINDOW, axis=2)
        wpos = s - WINDOW + jnp.arange(Q_BLOCK + WINDOW)
        wmask = (wpos[None, :] <= t[:, None]) & (wpos[None, :] > t[:, None] - WINDOW) & (wpos[None, :] >= 0)
        sc = jnp.einsum('bgrqd,bgkd->bgrqk', qb, k_wb) * scale
        p_w = masked_softmax(sc, wmask)
        o_w = jnp.einsum('bgrqk,bgkd->bgrqd', p_w.astype(v_wb.dtype), v_wb)
        o = gb[..., 0:1] * o_c + gb[..., 1:2] * o_s + gb[..., 2:3] * o_w
        return o.astype(q.dtype)

    out = lax.map(one_block, jnp.arange(n_qblk))
    return out.transpose(1, 0, 4, 2, 3, 5).reshape(B_, S_, NSA_W)


def memory_branch(qm, mem, ln_g, w_kv):
    B_, S_, _ = qm.shape
    M_ = mem.shape[1]
    kv = (rmsnorm(mem, ln_g) @ w_kv).reshape(B_, M_, 2, MEM_HEADS, MEM_HEAD_DIM)
    k, v = kv[:, :, 0], kv[:, :, 1]
    q = qm.reshape(B_, S_, MEM_HEADS, MEM_HEAD_DIM)
    sc = jnp.einsum('bshd,bmhd->bhsm', q, k) * (MEM_HEAD_DIM ** -0.5)
    p = jax.nn.softmax(sc.astype(jnp.float32), axis=-1)
    o = jnp.einsum('bhsm,bmhd->bshd', p.astype(v.dtype), v)
    return o.reshape(B_, S_, MEM_W)


def setup_inputs(seed: int = 0) -> dict:
    key = jax.random.key(seed)
    ks = iter(jax.random.split(key, 40))
    f32 = jnp.float32

    def nrm(shape, fan_in):
        return jax.random.normal(next(ks), shape, f32) * (fan_in ** -0.5)

    def gain(shape):
        return 1.0 + 0.05 * jax.random.normal(next(ks), shape, f32)

    def small(shape, s=0.02):
        return s * jax.random.normal(next(ks), shape, f32)

    x = jax.random.normal(next(ks), (BATCH, SEQ, D_MODEL), f32)
    mem = jax.random.normal(next(ks), (BATCH, MEM_LEN, D_MODEL), f32)
    positions = jnp.broadcast_to(jnp.arange(SEQ, dtype=jnp.int32)[None, :], (BATCH, SEQ))
    u = jax.random.uniform(next(ks), (DEPTH, D_RNN), f32, minval=0.9, maxval=0.999)
    sa = u ** (1.0 / LRU_C)
    lru_lambda = jnp.log(sa) - jnp.log1p(-sa)
    return {
        "x": x,
        "mem": mem,
        "positions": positions,
        "ln_mix_pre": gain((DEPTH, D_MODEL)),
        "w_in": nrm((DEPTH, D_MODEL, D_IN), D_MODEL),
        "conv_w": nrm((DEPTH, CONV_W, D_RNN), CONV_W),
        "conv_b": small((DEPTH, D_RNN)),
        "lru_wr": nrm((DEPTH, LRU_BLOCKS, LRU_BW, LRU_BW), LRU_BW),
        "lru_br": small((DEPTH, LRU_BLOCKS, LRU_BW)),
        "lru_wi": nrm((DEPTH, LRU_BLOCKS, LRU_BW, LRU_BW), LRU_BW),
        "lru_bi": small((DEPTH, LRU_BLOCKS, LRU_BW)),
        "lru_lambda": lru_lambda,
        "cmp_pe": small((DEPTH, 2, CMP_LEN, HEAD_DIM), 0.1),
        "cmp_w1": nrm((DEPTH, 2, CMP_LEN * HEAD_DIM, CMP_HIDDEN), CMP_LEN * HEAD_DIM),
        "cmp_b1": small((DEPTH, 2, CMP_HIDDEN)),
        "cmp_w2": nrm((DEPTH, 2, CMP_HIDDEN, HEAD_DIM), CMP_HIDDEN),
        "ln_mem": gain((DEPTH, D_MODEL)),
        "w_mem_kv": nrm((DEPTH, D_MODEL, 2 * MEM_W), D_MODEL),
        "w_br_rnn": nrm((DEPTH, D_RNN, D_MODEL), D_RNN),
        "w_br_nsa": nrm((DEPTH, NSA_W, D_MODEL), NSA_W),
        "w_br_mem": nrm((DEPTH, MEM_W, D_MODEL), MEM_W),
        "w_out": nrm((DEPTH, D_MODEL, D_MODEL), D_MODEL),
        "ln_mix_post": gain((DEPTH, D_MODEL)),
        "ln_mlp_pre": gain((DEPTH, D_MODEL)),
        "mlp_w1": nrm((DEPTH, D_MODEL, D_FF), D_MODEL),
        "mlp_w2": nrm((DEPTH, D_FF, D_MODEL), D_FF),
        "ln_mlp_post": gain((DEPTH, D_MODEL)),
    }


def reference(x, mem, positions, ln_mix_pre, w_in, conv_w, conv_b, lru_wr, lru_br, lru_wi, lru_bi,
              lru_lambda, cmp_pe, cmp_w1, cmp_b1, cmp_w2, ln_mem, w_mem_kv, w_br_rnn, w_br_nsa,
              w_br_mem, w_out, ln_mix_post, ln_mlp_pre, mlp_w1, mlp_w2, ln_mlp_post):
    for l in range(DEPTH):
        h = rmsnorm(x, ln_mix_pre[l])
        proj = h @ w_in[l]
        xr, yr, q, kv, g_nsa, qm, g_merge = jnp.split(proj, IN_OFFSETS, axis=-1)
        o_a = rglru_branch(xr, yr, conv_w[l], conv_b[l], lru_wr[l], lru_br[l],
                           lru_wi[l], lru_bi[l], lru_lambda[l])
        o_b = nsa_branch(q, kv, g_nsa, positions, cmp_pe[l], cmp_w1[l], cmp_b1[l], cmp_w2[l])
        o_c = memory_branch(qm, mem, ln_mem[l], w_mem_kv[l])
        ga, gb, gc = jnp.split(jax.nn.sigmoid(g_merge), 3, axis=-1)
        merged = ga * (o_a @ w_br_rnn[l]) + gb * (o_b @ w_br_nsa[l]) + gc * (o_c @ w_br_mem[l])
        x = x + rmsnorm(merged @ w_out[l], ln_mix_post[l])
        h = rmsnorm(x, ln_mlp_pre[l])
        x = x + rmsnorm(jnp.square(jax.nn.relu(h @ mlp_w1[l])) @ mlp_w2[l], ln_mlp_post[l])
    return x
```

```python
import numpy as np
import ml_dtypes
from contextlib import ExitStack, contextmanager

import concourse.bass as bass
import concourse.mybir as mybir
from concourse.bass_utils import run_bass_kernel_spmd

F32 = mybir.dt.float32
BF16 = mybir.dt.bfloat16
I32 = mybir.dt.int32
AF = mybir.ActivationFunctionType
ALU = mybir.AluOpType
AX = mybir.AxisListType
NPBF = ml_dtypes.bfloat16

D = 1024
DEPTH = 4
SEQ = 8192
BATCH = 2
NCORE = 8
EPS = 1e-6
D_IN = 8752
OFF_XR, OFF_YR, OFF_Q, OFF_KV, OFF_G, OFF_QM, OFF_GM = 0, 1024, 2048, 3072, 4608, 4656, 5680
GELU_C = 1.5957691216057308


class _Op:
    __slots__ = ("stream", "dom", "fn", "deps", "inc", "cnt")

    def __init__(self, stream, dom, fn, deps):
        self.stream = stream
        self.dom = dom
        self.fn = fn
        self.deps = deps
        self.inc = False
        self.cnt = 0


class Sched:
    STREAMS = ("pe", "act", "dve", "pool", "sp")

    def __init__(self, nc):
        self.nc = nc
        self.stream_ops = {s: [] for s in self.STREAMS}
        self.dom_ops = {}
        self.last_w = {}
        self.readers = {}
        self.es = ExitStack()
        self._n = 0
        self._stack = [self.es]
        self._dma_rr = {}

    def sb(self, shape, dt, name=None):
        self._n += 1
        return self._stack[-1].enter_context(
            self.nc.sbuf_tensor(f"{name or 'sb'}_{self._n}", list(shape), dt))

    def ps(self, shape, dt=F32, name=None):
        self._n += 1
        return self._stack[-1].enter_context(
            self.nc.psum_tensor(f"{name or 'ps'}_{self._n}", list(shape), dt))

    @contextmanager
    def scope(self):
        es = ExitStack()
        self._stack.append(es)
        try:
            yield
        finally:
            self._stack.pop()
            self.barrier()
            es.close()

    def barrier(self):
        last = {d: len(l) - 1 for d, l in self.dom_ops.items() if l}
        for s in self.STREAMS:
            if not self.stream_ops[s] and s not in ("pe", "act", "dve"):
                continue
            deps = {d: i for d, i in last.items() if d != s}
            op = _Op(s, s, lambda e: e.nop(), deps)
            for d, i in deps.items():
                self.dom_ops[d][i].inc = True
            self.dom_ops.setdefault(s, []).append(op)
            self.stream_ops[s].append(op)

    NDMA_SEM = 8

    def add(self, stream, fn, reads=(), writes=(), dma=False):
        if dma:
            k = self._dma_rr.get(stream, 0)
            self._dma_rr[stream] = k + 1
            dom = f"dma_{stream}_{k % self.NDMA_SEM}"
        else:
            dom = stream
        dl = self.dom_ops.setdefault(dom, [])
        idx = len(dl)
        deps = {}
        if dma and idx > 0:
            deps[dom] = idx - 1

        def dep(d, i, kind):
            if d == dom and not dma:
                if kind != "raw" or dom == "pe":
                    return
            if deps.get(d, -1) < i:
                deps[d] = i

        for t in reads:
            lw = self.last_w.get(t)
            if lw is not None:
                dep(lw[0], lw[1], "raw")
        for t in writes:
            lw = self.last_w.get(t)
            if lw is not None:
                dep(lw[0], lw[1], "waw")
            rd = self.readers.get(t)
            if rd:
                for d, i in rd.items():
                    dep(d, i, "war")
        op = _Op(stream, dom, fn, deps)
        if dma:
            op.inc = True
        dl.append(op)
        self.stream_ops[stream].append(op)
        for d, i in deps.items():
            self.dom_ops[d][i].inc = True
        for t in reads:
            self.readers.setdefault(t, {})[dom] = idx
        for t in writes:
            self.last_w[t] = (dom, idx)
            self.readers[t] = {}
        return op

    def pe(self, fn, reads=(), writes=()):
        return self.add("pe", fn, reads, writes)

    def act(self, fn, reads=(), writes=()):
        return self.add("act", fn, reads, writes)

    def dve(self, fn, reads=(), writes=()):
        return self.add("dve", fn, reads, writes)

    def gp(self, fn, reads=(), writes=()):
        return self.add("pool", fn, reads, writes)

    def dma(self, out, in_, reads=(), writes=(), q="sp"):
        return self.add(q, lambda e: e.dma_start(out=out, in_=in_), reads, writes, dma=True)

    def finish(self, final_ops=()):
        for d, dl in self.dom_ops.items():
            c = 0
            for op in dl:
                if op.inc:
                    c += 1
                op.cnt = c
        nc = self.nc
        doms = list(self.dom_ops.keys())
        sems = {d: self.es.enter_context(nc.semaphore("s_" + d)) for d in doms}
        step = {d: (16 if d.startswith("dma_") else 1) for d in doms}
        finals = {}
        for op in final_ops:
            f = finals.setdefault(op.stream, {})
            f[op.dom] = max(f.get(op.dom, 0), op.cnt)
        dom_ops = self.dom_ops
        stream_ops = self.stream_ops

        def emit(stream, eng):
            known = {}
            for op in stream_ops[stream]:
                for d, i in op.deps.items():
                    v = dom_ops[d][i].cnt * step[d]
                    if known.get(d, 0) < v:
                        eng.wait_ge(sems[d], v)
                        known[d] = v
                ins = op.fn(eng)
                if op.inc:
                    ins.then_inc(sems[op.dom], step[op.dom])
            for d, c in finals.get(stream, {}).items():
                eng.wait_ge(sems[d], c * step[d])

        with nc.Block() as block:
            if stream_ops["sp"]:
                @block.sync
                def _(e):
                    emit("sp", e)
            if stream_ops["pe"]:
                @block.tensor
                def _(e):
                    emit("pe", e)
            if stream_ops["act"]:
                @block.scalar
                def _(e):
                    emit("act", e)
            if stream_ops["dve"]:
                @block.vector
                def _(e):
                    emit("dve", e)
            if stream_ops["pool"]:
                @block.gpsimd
                def _(e):
                    emit("pool", e)
        self.es.close()


class Rot:
    def __init__(self, S, n, shape, dt, name, psum=False):
        self.name = name
        self.bufs = [(S.ps(shape, dt, name) if psum else S.sb(shape, dt, name)) for _ in range(n)]
        self.i = -1

    def next(self):
        self.i = (self.i + 1) % len(self.bufs)
        return self.bufs[self.i], (self.name, self.i)


def _dram_in(nc, name, shape, dt):
    return nc.dram_tensor(name, list(shape), dt, kind="ExternalInput").ap()


def _dram_out(nc, name, shape, dt):
    return nc.dram_tensor(name, list(shape), dt, kind="ExternalOutput").ap()


class WStream:
    def __init__(self, S, nw=5, nst=2):
        self.S = S
        self.w = Rot(S, nw, [128, 4096], BF16, "wt")
        self.st = Rot(S, nst, [128, 2048], F32, "wst")
        self.qi = 0

    def load(self, src_fn, kc, ncol):
        S = self.S
        assert kc * ncol == 4096
        wt, wtok = self.w.next()
        wv = wt[:, :].rearrange("p (c n) -> p c n", c=kc)
        half = kc // 2
        for hh in range(2):
            st, stok = self.st.next()
            sv = st[:, :].rearrange("p (c n) -> p c n", c=half)
            self.qi += 1
            S.dma(sv, src_fn(hh * half, (hh + 1) * half), writes=[stok],
                  q=("sp" if self.qi % 2 else "pool"))
            S.gp(lambda e, o=wv[:, hh * half:(hh + 1) * half, :], i=sv: e.tensor_copy(out=o, in_=i),
                 reads=[stok], writes=[wtok])
        return wv, wtok


def w_src(W, col0, ncol):
    Wv = W.rearrange("(c p) n -> p c n", p=128)
    return lambda c0, c1: Wv[:, c0:c1, col0:col0 + ncol]


def emit_rstd(S, K, x_ap, x_toks, sq, sq_tok, ps, ps_tok, rstd, rstd_tok, nch, n, dscale):
    S.act(lambda e: e.activation(out=sq[:, 0:nch, 0:n], in_=x_ap, func=AF.Square),
          reads=x_toks, writes=[sq_tok])
    for c in range(nch):
        S.pe(lambda e, c=c: e.matmul(ps[:, 0:n], K["ones"][:, :], sq[:, c, 0:n],
                                     start=(c == 0), stop=(c == nch - 1)),
             reads=[sq_tok, "ones"], writes=[ps_tok])
    S.act(lambda e: e.activation(out=rstd[:, 0:n], in_=ps[:, 0:n], func=AF.Sqrt,
                                 bias=K["eps"][:, 0:1], scale=dscale),
          reads=[ps_tok, "eps"], writes=[rstd_tok])
    S.dve(lambda e: e.reciprocal(out=rstd[:, 0:n], in_=rstd[:, 0:n]),
          reads=[rstd_tok], writes=[rstd_tok])


def emit_consts(S, K):
    K["ones"] = S.sb([128, 128], BF16, "ones")
    K["eps"] = S.sb([128, 1], F32, "eps")
    S.dve(lambda e: e.memset(K["ones"][:, :], 1.0), writes=["ones"])
    S.dve(lambda e: e.memset(K["eps"][:, :], EPS), writes=["eps"])


def build_A(T):
    nc = bass.Bass("TRN2", target_bir_lowering=False)
    xT = _dram_in(nc, "xT", [D, T], F32)
    g = _dram_in(nc, "g", [128, 8], F32)
    hT = _dram_out(nc, "hT", [D, T], BF16)
    S = Sched(nc)
    K = {}
    emit_consts(S, K)
    gt = S.sb([128, 8], F32, "g")
    S.dma(gt[:, :], g, writes=["g"])
    xr = Rot(S, 2, [128, 8, 512], F32, "x")
    hr = Rot(S, 2, [128, 8, 512], BF16, "h")
    sq = S.sb([128, 8, 512], BF16, "sq")
    rr = Rot(S, 2, [128, 512], F32, "rstd")
    pr = Rot(S, 2, [128, 512], F32, "ps", psum=True)
    xTv = xT.rearrange("(c p) t -> p c t", p=128)
    hTv = hT.rearrange("(c p) t -> p c t", p=128)
    fins = []
    for b in range(T // 512):
        xt, xtok = xr.next()
        S.dma(xt[:, :, :], xTv[:, :, b * 512:(b + 1) * 512], writes=[xtok])
        ps, pstok = pr.next()
        rs, rstok = rr.next()
        emit_rstd(S, K, xt[:, :, :], [xtok], sq, "sq", ps, pstok, rs, rstok, 8, 512, 1.0 / D)
        ht, htok = hr.next()
        for c in range(8):
            S.dve(lambda e, c=c, ht=ht, xt=xt, rs=rs: e.scalar_tensor_tensor(
                out=ht[:, c, :], in0=xt[:, c, :], scalar=gt[:, c:c + 1], in1=rs[:, :],
                op0=ALU.mult, op1=ALU.mult), reads=[xtok, rstok, "g"], writes=[htok])
        fins.append(S.dma(hTv[:, :, b * 512:(b + 1) * 512], ht[:, :, :], reads=[htok], q="pool"))
    S.finish(fins)
    return nc


G_PRE, G_MEM, G_POST, G_MLPPRE, G_MLPPOST, G_NEXT = range(6)


def build_C(T):
    nc = bass.Bass("TRN2", target_bir_lowering=False)
    xT = _dram_in(nc, "xT", [D, T], F32)
    oaT = _dram_in(nc, "oaT", [D, T], BF16)
    obT = _dram_in(nc, "obT", [D, T], BF16)
    memT = _dram_in(nc, "memT", [D, 256], F32)
    gains = _dram_in(nc, "gains", [128, 48], F32)
    w_qm = _dram_in(nc, "w_qm", [D, 1024], F32)
    w_gm = _dram_in(nc, "w_gm", [D, 3072], F32)
    w_kv = _dram_in(nc, "w_kv", [D, 2048], F32)
    w_br = [_dram_in(nc, n, [D, D], F32) for n in ("w_ra", "w_rb", "w_rc")]
    w_out = _dram_in(nc, "w_out", [D, D], F32)
    w1 = _dram_in(nc, "w1", [D, 4096], F32)
    w2 = _dram_in(nc, "w2", [4096, D], F32)
    xoT = _dram_out(nc, "xoT", [D, T], F32)
    hnT = _dram_out(nc, "hnT", [D, T], BF16)

    S = Sched(nc)
    K = {}
    emit_consts(S, K)
    gt = S.sb([128, 48], F32, "gains")
    S.dma(gt[:, :], gains, writes=["g"])

    def gain(gi, c):
        return gt[:, gi * 8 + c:gi * 8 + c + 1]

    W = WStream(S, nw=4, nst=2)
    PS = Rot(S, 6, [128, 512], F32, "ps", psum=True)
    kmemT = S.sb([128, 8, 256], BF16, "kmemT")
    vmem = S.sb([128, 2, 1024], BF16, "vmem")
    sq = S.sb([128, 8, 512], BF16, "sq")
    RS = Rot(S, 2, [128, 512], F32, "rstd")

    with S.scope():
        mt32 = S.sb([128, 8, 256], F32, "mem32")
        memn = S.sb([128, 8, 256], BF16, "memn")
        S.dma(mt32[:, :, :], memT.rearrange("(c p) t -> p c t", p=128), writes=["mem32"])
        ps, pstok = PS.next()
        rs, rstok = RS.next()
        emit_rstd(S, K, mt32[:, :, :], ["mem32"], sq, "sq", ps, pstok, rs, rstok, 8, 256, 1.0 / D)
        for c in range(8):
            S.dve(lambda e, c=c, rs=rs: e.scalar_tensor_tensor(
                out=memn[:, c, :], in0=mt32[:, c, :], scalar=gain(G_MEM, c), in1=rs[:, 0:256],
                op0=ALU.mult, op1=ALU.mult), reads=["mem32", rstok, "g"], writes=["memn"])
        for cg in range(4):
            wv, wtok = W.load(w_src(w_kv, cg * 512, 512), 8, 512)
            if cg < 2:
                for mi in range(4):
                    m = cg * 4 + mi
                    ps, pstok = PS.next()
                    for c in range(8):
                        S.pe(lambda e, c=c, ps=ps, wv=wv, mi=mi: e.matmul(
                            ps[:, 0:256], wv[:, c, mi * 128:(mi + 1) * 128], memn[:, c, :],
                            start=(c == 0), stop=(c == 7)), reads=[wtok, "memn"], writes=[pstok])
                    S.act(lambda e, ps=ps, m=m: e.activation(out=kmemT[:, m, :], in_=ps[:, 0:256], func=AF.Copy),
                          reads=[pstok], writes=["kmemT"])
            else:
                for mt in range(2):
                    ps, pstok = PS.next()
                    for c in range(8):
                        S.pe(lambda e, c=c, ps=ps, wv=wv, mt=mt: e.matmul(
                            ps[:, :], memn[:, c, mt * 128:(mt + 1) * 128], wv[:, c, :],
                            start=(c == 0), stop=(c == 7)), reads=[wtok, "memn"], writes=[pstok])
                    S.act(lambda e, ps=ps, mt=mt, cg=cg: e.activation(
                        out=vmem[:, mt, (cg - 2) * 512:(cg - 1) * 512], in_=ps[:, :], func=AF.Copy),
                        reads=[pstok], writes=["vmem"])

    xt = S.sb([128, 8, 512], F32, "x")
    Bf = [S.sb([128, 8, 512], BF16, f"B{i}") for i in range(5)]
    y = S.sb([128, 8, 512], F32, "y")
    u = S.sb([128, 32, 512], BF16, "u")
    acc = S.sb([128, 4, 512], F32, "acc")
    GT = Rot(S, 2, [128, 512], F32, "gate")
    TM = Rot(S, 2, [128, 512], F32, "tmp")
    ER = Rot(S, 2, [128, 2, 512], BF16, "E")
    Rr = S.sb([128, 512], F32, "R")
    xTv = xT.rearrange("(c p) t -> p c t", p=128)
    oaTv = oaT.rearrange("(c p) t -> p c t", p=128)
    obTv = obT.rearrange("(c p) t -> p c t", p=128)
    xoTv = xoT.rearrange("(c p) t -> p c t", p=128)
    hnTv = hnT.rearrange("(c p) t -> p c t", p=128)
    w2v = w2.rearrange("(c p) n -> p c n", p=128)
    XT = [("x", c) for c in range(8)]
    YT = [("y", c) for c in range(8)]
    fins = []

    def norm_to_bf16(dst, dtok, gi):
        ps, pstok = PS.next()
        rs, rstok = RS.next()
        emit_rstd(S, K, xt[:, :, :], XT, sq, "sq", ps, pstok, rs, rstok, 8, 512, 1.0 / D)
        for c in range(8):
            S.dve(lambda e, c=c, rs=rs: e.scalar_tensor_tensor(
                out=dst[:, c, :], in0=xt[:, c, :], scalar=gain(gi, c), in1=rs[:, :],
                op0=ALU.mult, op1=ALU.mult), reads=[("x", c), rstok, "g"], writes=[dtok])

    def linear(wsrc_fn, kc, ncol, rhs_fn, rhs_toks_fn, nm, consume):
        wv, wtok = W.load(wsrc_fn, kc, ncol)
        for mi in range(ncol // 128):
            ps, pstok = PS.next()
            for c in range(kc):
                S.pe(lambda e, c=c, ps=ps, wv=wv, mi=mi: e.matmul(
                    ps[:, :], wv[:, c, mi * 128:(mi + 1) * 128], rhs_fn(c),
                    start=(c == 0), stop=(c == kc - 1)),
                    reads=[wtok] + rhs_toks_fn(c), writes=[pstok])
            consume(mi, ps, pstok)

    def post_norm_residual(gi):
        ps, pstok = PS.next()
        rs, rstok = RS.next()
        emit_rstd(S, K, y[:, :, :], YT, sq, "sq", ps, pstok, rs, rstok, 8, 512, 1.0 / D)
        for c in range(8):
            S.dve(lambda e, c=c, rs=rs: e.scalar_tensor_tensor(
                out=y[:, c, :], in0=y[:, c, :], scalar=gain(gi, c), in1=rs[:, :],
                op0=ALU.mult, op1=ALU.mult), reads=[("y", c), rstok, "g"], writes=[("y", c)])
            S.gp(lambda e, c=c: e.tensor_tensor(out=xt[:, c, :], in0=xt[:, c, :], in1=y[:, c, :], op=ALU.add),
                 reads=[("y", c), ("x", c)], writes=[("x", c)])

    for p in range(T // 512):
        blk = slice(p * 512, (p + 1) * 512)
        S.dma(xt[:, :, :], xTv[:, :, blk], writes=XT)
        S.dma(Bf[3][:, :, :], oaTv[:, :, blk], writes=["B3"], q="pool")
        S.dma(Bf[4][:, :, :], obTv[:, :, blk], writes=["B4"], q="pool")
        norm_to_bf16(Bf[0], "B0", G_PRE)
        for cg in range(2):
            def cons(mi, ps, pstok, cg=cg):
                m = cg * 4 + mi
                S.act(lambda e: e.activation(out=Bf[1][:, m, :], in_=ps[:, :], func=AF.Copy),
                      reads=[pstok], writes=[("B1", m)])
            linear(w_src(w_qm, cg * 512, 512), 8, 512, lambda c: Bf[0][:, c, :], lambda c: ["B0"], 4, cons)
        for hh in range(4):
            Et, Etok = ER.next()
            for mt in range(2):
                ps, pstok = PS.next()
                for cc in range(2):
                    S.pe(lambda e, cc=cc, ps=ps, mt=mt, hh=hh: e.matmul(
                        ps[:, :], kmemT[:, 2 * hh + cc, mt * 128:(mt + 1) * 128], Bf[1][:, 2 * hh + cc, :],
                        start=(cc == 0), stop=(cc == 1)),
                        reads=["kmemT", ("B1", 2 * hh + cc)], writes=[pstok])
                S.act(lambda e, ps=ps, Et=Et, mt=mt: e.activation(
                    out=Et[:, mt, :], in_=ps[:, :], func=AF.Exp, scale=1.0 / 16.0),
                    reads=[pstok], writes=[(Etok, mt)])
            ps, pstok = PS.next()
            for mt in range(2):
                S.pe(lambda e, ps=ps, Et=Et, mt=mt: e.matmul(
                    ps[:, :], K["ones"][:, :], Et[:, mt, :], start=(mt == 0), stop=(mt == 1)),
                    reads=["ones", (Etok, mt)], writes=[pstok])
            S.dve(lambda e, ps=ps: e.reciprocal(out=Rr[:, :], in_=ps[:, :]), reads=[pstok], writes=["R"])
            for dh in range(2):
                ps, pstok = PS.next()
                for mt in range(2):
                    S.pe(lambda e, ps=ps, Et=Et, mt=mt, hh=hh, dh=dh: e.matmul(
                        ps[:, :], vmem[:, mt, hh * 256 + dh * 128: hh * 256 + (dh + 1) * 128], Et[:, mt, :],
                        start=(mt == 0), stop=(mt == 1)),
                        reads=["vmem", (Etok, mt)], writes=[pstok])
                S.dve(lambda e, ps=ps, hh=hh, dh=dh: e.tensor_tensor(
                    out=Bf[2][:, 2 * hh + dh, :], in0=ps[:, :], in1=Rr[:, :], op=ALU.mult),
                    reads=[pstok, "R"], writes=[("B2", 2 * hh + dh)])
        srcs = [(Bf[3], lambda c: ["B3"]), (Bf[4], lambda c: ["B4"]), (Bf[2], lambda c: [("B2", c)])]
        for cg in range(2):
            for j in range(3):
                gates = {}

                def cons_g(mi, ps, pstok, gates=gates):
                    g_, gtok = GT.next()
                    S.act(lambda e: e.activation(out=g_[:, :], in_=ps[:, :], func=AF.Sigmoid),
                          reads=[pstok], writes=[gtok])
                    gates[mi] = (g_, gtok)

                wg, wgtok = W.load(w_src(w_gm, j * 1024 + cg * 512, 512), 8, 512)
                wr, wrtok = W.load(w_src(w_br[j], cg * 512, 512), 8, 512)
                src, stoks = srcs[j]
                for mi in range(4):
                    m = cg * 4 + mi
                    ps, pstok = PS.next()
                    for c in range(8):
                        S.pe(lambda e, c=c, ps=ps, mi=mi, wg=wg: e.matmul(
                            ps[:, :], wg[:, c, mi * 128:(mi + 1) * 128], Bf[0][:, c, :],
                            start=(c == 0), stop=(c == 7)), reads=[wgtok, "B0"], writes=[pstok])
                    g_, gtok = GT.next()
                    S.act(lambda e, ps=ps, g_=g_: e.activation(out=g_[:, :], in_=ps[:, :], func=AF.Sigmoid),
                          reads=[pstok], writes=[gtok])
                    ps2, ps2tok = PS.next()
                    for c in range(8):
                        S.pe(lambda e, c=c, ps2=ps2, mi=mi, wr=wr, src=src: e.matmul(
                            ps2[:, :], wr[:, c, mi * 128:(mi + 1) * 128], src[:, c, :],
                            start=(c == 0), stop=(c == 7)), reads=[wrtok] + stoks(c), writes=[ps2tok])
                    if j == 0:
                        S.dve(lambda e, ps2=ps2, g_=g_, mi=mi: e.tensor_tensor(
                            out=acc[:, mi, :], in0=ps2[:, :], in1=g_[:, :], op=ALU.mult),
                            reads=[ps2tok, gtok], writes=[("acc", mi)])
                    else:
                        tm, tmtok = TM.next()
                        S.dve(lambda e, ps2=ps2, g_=g_, tm=tm: e.tensor_tensor(
                            out=tm[:, :], in0=ps2[:, :], in1=g_[:, :], op=ALU.mult),
                            reads=[ps2tok, gtok], writes=[tmtok])
                        if j == 1:
                            S.gp(lambda e, tm=tm, mi=mi: e.tensor_tensor(
                                out=acc[:, mi, :], in0=acc[:, mi, :], in1=tm[:, :], op=ALU.add),
                                reads=[tmtok, ("acc", mi)], writes=[("acc", mi)])
                        else:
                            S.gp(lambda e, tm=tm, mi=mi, m=m: e.tensor_tensor(
                                out=Bf[1][:, m, :], in0=acc[:, mi, :], in1=tm[:, :], op=ALU.add),
                                reads=[tmtok, ("acc", mi)], writes=[("B1", m)])
        for cg in range(2):
            def cons(mi, ps, pstok, cg=cg):
                m = cg * 4 + mi
                S.act(lambda e: e.activation(out=y[:, m, :], in_=ps[:, :], func=AF.Copy),
                      reads=[pstok], writes=[("y", m)])
            linear(w_src(w_out, cg * 512, 512), 8, 512, lambda c: Bf[1][:, c, :], lambda c: [("B1", c)], 4, cons)
        post_norm_residual(G_POST)
        norm_to_bf16(Bf[0], "B0", G_MLPPRE)
        for cg in range(8):
            def cons(mi, ps, pstok, cg=cg):
                f = cg * 4 + mi
                tm, tmtok = TM.next()
                S.act(lambda e: e.activation(out=tm[:, :], in_=ps[:, :], func=AF.Relu),
                      reads=[pstok], writes=[tmtok])
                S.dve(lambda e: e.tensor_tensor(out=u[:, f, :], in0=tm[:, :], in1=tm[:, :], op=ALU.mult),
                      reads=[tmtok], writes=[("u", f)])
            linear(w_src(w1, cg * 512, 512), 8, 512, lambda c: Bf[0][:, c, :], lambda c: ["B0"], 4, cons)
        for m in range(8):
            def cons(mi, ps, pstok, m=m):
                S.act(lambda e: e.activation(out=y[:, m, :], in_=ps[:, :], func=AF.Copy),
                      reads=[pstok], writes=[("y", m)])
            linear(lambda c0, c1, m=m: w2v[:, c0:c1, m * 128:(m + 1) * 128], 32, 128,
                   lambda c: u[:, c, :], lambda c: [("u", c)], 1, cons)
        post_norm_residual(G_MLPPOST)
        fins.append(S.dma(xoTv[:, :, blk], xt[:, :, :], reads=XT))
        norm_to_bf16(Bf[0], "B0", G_NEXT)
        fins.append(S.dma(hnTv[:, :, blk], Bf[0][:, :, :], reads=["B0"], q="pool"))
    S.finish(fins)
    return nc


TWO_PI = 2.0 * np.pi
CW1 = 6.28125
CW2 = float(np.float32(0.0019353032112121582))
CW3 = float(TWO_PI - CW1 - CW2)
SCALE = 0.125
NQT = SEQ // 128


def build_B(stop_after=3, debug=False):
    nc = bass.Bass("TRN2", target_bir_lowering=False)
    hT = _dram_in(nc, "hT", [D, SEQ], BF16)
    pos = _dram_in(nc, "pos", [1, SEQ], I32)
    w_rg = _dram_in(nc, "w_rg", [D, 512], F32)
    rgp = _dram_in(nc, "rgp", [128, 2, 16], F32)
    lruw = _dram_in(nc, "lruw", [128, 2, 2, 128], F32)
    w_q = _dram_in(nc, "w_q", [D, 1024], F32)
    w_k = _dram_in(nc, "w_k", [D, 512], F32)
    w_g = _dram_in(nc, "w_g", [D, 12], F32)
    rc = _dram_in(nc, "rc", [128, 4], F32)
    cw1 = _dram_in(nc, "cw1", [2, 64, 32, 256], F32)
    cpe = _dram_in(nc, "cpe", [64, 2, 32], F32)
    cb1 = _dram_in(nc, "cb1", [128, 2, 2], F32)
    cw2 = _dram_in(nc, "cw2", [128, 2, 192], F32)
    ovl = _dram_in(nc, "ovl", [128, 4, 128], BF16)
    cmask = _dram_in(nc, "cmask", [128, 17, 128], BF16)
    tri = _dram_in(nc, "tri", [128, 2, 128], BF16)
    ident = _dram_in(nc, "ident", [128, 128], BF16)
    bw = _dram_in(nc, "bw", [128, 256], F32)
    oaT = _dram_out(nc, "oaT", [256, SEQ], BF16)
    ob = _dram_out(nc, "ob", [SEQ, 256], BF16)

    S = Sched(nc)
    K = {}
    emit_consts(S, K)
    one = S.sb([128, 1], F32, "one")
    S.dve(lambda e: e.memset(one[:, :], 1.0), writes=["one"])
    hTv = hT.rearrange("(c p) t -> p c t", p=128)
    HB = Rot(S, 2, [128, 8, 512], BF16, "hblk")
    G = Rot(S, 3, [128, 512], F32, "psg", psum=True)
    fins = []
    hq = [0]

    def load_h(tb):
        hb, htok = HB.next()
        hq[0] += 1
        S.dma(hb[:, :, :], hTv[:, :, tb * 512:(tb + 1) * 512], writes=[htok],
              q=("sp" if hq[0] % 2 else "pool"))
        return hb, htok

    wrg_b = S.sb([128, 8, 512], BF16, "wrg")
    wq_b = [S.sb([128, 8, 512], BF16, f"wq{i}") for i in range(2)]
    wk_b = S.sb([128, 8, 512], BF16, "wk")
    wg_b = S.sb([128, 8, 12], BF16, "wg")
    lru_b = S.sb([128, 2, 2, 128], BF16, "lruw")
    rgp_t = S.sb([128, 2, 16], F32, "rgp")
    rc_t = S.sb([128, 4], F32, "rc")
    cb1_t = S.sb([128, 2, 2], F32, "cb1")
    cw2_b = S.sb([128, 2, 192], BF16, "cw2")
    cpe_b = S.sb([64, 2, 32], BF16, "cpe")
    ovl_t = S.sb([128, 4, 128], BF16, "ovl")
    cmask_t = S.sb([128, 17, 128], BF16, "cmask")
    tri_t = S.sb([128, 2, 128], BF16, "tri")
    ident_t = S.sb([128, 128], BF16, "ident")
    bw_t = S.sb([128, 256], F32, "bw")
    S.dma(rgp_t[:, :, :], rgp, writes=["rgp"])
    S.dma(rc_t[:, :], rc, writes=["rc"])
    S.dma(cb1_t[:, :, :], cb1, writes=["cb1"])
    S.dma(ovl_t[:, :, :], ovl, writes=["ovl"])
    S.dma(cmask_t[:, :, :], cmask, writes=["cmask"])
    S.dma(tri_t[:, :, :], tri, writes=["tri"])
    S.dma(ident_t[:, :], ident, writes=["ident"])
    S.dma(bw_t[:, :], bw, writes=["bw"])
    with S.scope():
        st = Rot(S, 2, [128, 8, 512], F32, "stg")

        def cast_in(dst, dtok, src, shape_sl):
            s_, stok = st.next()
            sv = shape_sl(s_)
            S.dma(sv, src, writes=[stok])
            S.gp(lambda e: e.tensor_copy(out=dst, in_=sv), reads=[stok], writes=[dtok])

        wv = lambda W_: W_.rearrange("(c p) n -> p c n", p=128)
        cast_in(wrg_b[:, :, :], "wrg", wv(w_rg), lambda s_: s_[:, :, :])
        cast_in(wq_b[0][:, :, :], "wq0", wv(w_q)[:, :, 0:512], lambda s_: s_[:, :, :])
        cast_in(wq_b[1][:, :, :], "wq1", wv(w_q)[:, :, 512:1024], lambda s_: s_[:, :, :])
        cast_in(wk_b[:, :, :], "wk", wv(w_k), lambda s_: s_[:, :, :])
        cast_in(wg_b[:, :, :], "wg", wv(w_g), lambda s_: s_[:, :, 0:12])
        cast_in(lru_b[:, :, :, :], "lruw", lruw,
                lambda s_: s_[:, 0, :].rearrange("p (a b c) -> p a b c", a=2, b=2))
        cast_in(cw2_b[:, :, :], "cw2", cw2, lambda s_: s_[:, 0, 0:384].rearrange("p (a b) -> p a b", a=2))
        cast_in(cpe_b[:, :, :], "cpe", cpe, lambda s_: s_[0:64, 0, 0:64].rearrange("p (a b) -> p a b", a=2))

    TCH = 2048
    oaTv = oaT
    with S.scope():
        xrp = S.sb([128, 3 + TCH], F32, "xrp")
        yg = S.sb([128, TCH], F32, "yg")
        xc = S.sb([128, TCH], F32, "xc")
        r_ = S.sb([128, TCH], F32, "r")
        i_ = S.sb([128, TCH], F32, "i")
        a_ = S.sb([128, TCH], F32, "a")
        t_ = S.sb([128, TCH], F32, "t")
        xcb = S.sb([128, TCH], BF16, "xcb")
        oab = Rot(S, 2, [128, TCH], BF16, "oab")
        cch = S.sb([128, 2, 2], F32, "cch")
        carry = S.sb([128, 1], F32, "carry")
        for cbi in range(2):
            lam = rgp_t[:, cbi, 7:8]
            S.act(lambda e, lam=lam, cbi=cbi: e.activation(out=cch[:, cbi, 0:1], in_=lam, func=AF.Exp, scale=-1.0),
                  reads=["rgp"], writes=["cch"])
            S.act(lambda e, cbi=cbi: e.activation(out=cch[:, cbi, 0:1], in_=cch[:, cbi, 0:1], func=AF.Ln,
                                                  bias=one[:, 0:1], scale=1.0),
                  reads=["cch", "one"], writes=["cch"])
            S.dve(lambda e, cbi=cbi: e.tensor_scalar(out=cch[:, cbi, 1:2], in0=cch[:, cbi, 0:1], scalar1=-16.0,
                                                     scalar2=None, op0=ALU.mult), reads=["cch"], writes=["cch"])
            S.dve(lambda e, cbi=cbi: e.tensor_scalar(out=cch[:, cbi, 0:1], in0=cch[:, cbi, 0:1], scalar1=-8.0,
                                                     scalar2=None, op0=ALU.mult), reads=["cch"], writes=["cch"])
        for cbi in range(2):
            pr = lambda k, cbi=cbi: rgp_t[:, cbi, k:k + 1]
            for ch in range(SEQ // TCH):
                if ch == 0:
                    S.dve(lambda e: e.memset(xrp[:, 0:3], 0.0), writes=["xrp"])
                    S.dve(lambda e: e.memset(carry[:, :], 0.0), writes=["carry"])
                else:
                    S.dve(lambda e: e.tensor_copy(out=xrp[:, 0:3], in_=xrp[:, TCH:TCH + 3]),
                          reads=["xrp"], writes=["xrp"])
                for sbk in range(TCH // 512):
                    hb, htok = load_h(ch * (TCH // 512) + sbk)
                    cs = slice(sbk * 512, (sbk + 1) * 512)
                    ps, pstok = G.next()
                    for c in range(8):
                        S.pe(lambda e, c=c, ps=ps, hb=hb, cbi=cbi: e.matmul(
                            ps[:, :], wrg_b[:, c, cbi * 128:(cbi + 1) * 128], hb[:, c, :],
                            start=(c == 0), stop=(c == 7)), reads=["wrg", htok], writes=[pstok])
                    S.act(lambda e, ps=ps, sbk=sbk: e.activation(
                        out=xrp[:, 3 + sbk * 512: 3 + (sbk + 1) * 512], in_=ps[:, :], func=AF.Copy),
                        reads=[pstok], writes=["xrp"])
                    ps, pstok = G.next()
                    for c in range(8):
                        S.pe(lambda e, c=c, ps=ps, hb=hb, cbi=cbi: e.matmul(
                            ps[:, :], wrg_b[:, c, 256 + cbi * 128:256 + (cbi + 1) * 128], hb[:, c, :],
                            start=(c == 0), stop=(c == 7)), reads=["wrg", htok], writes=[pstok])
                    S.act(lambda e, ps=ps, cs=cs: e.activation(out=yg[:, cs], in_=ps[:, :], func=AF.Copy),
                          reads=[pstok], writes=["yg"])
                S.act(lambda e: e.activation(out=t_[:, :], in_=yg[:, :], func=AF.Square), reads=["yg"], writes=["t"])
                S.dve(lambda e: e.tensor_scalar(out=t_[:, :], in0=t_[:, :], scalar1=0.044715, scalar2=1.0,
                                                op0=ALU.mult, op1=ALU.add), reads=["t"], writes=["t"])
                S.dve(lambda e: e.tensor_tensor(out=t_[:, :], in0=t_[:, :], in1=yg[:, :], op=ALU.mult),
                      reads=["t", "yg"], writes=["t"])
                S.act(lambda e: e.activation(out=t_[:, :], in_=t_[:, :], func=AF.Sigmoid, scale=GELU_C),
                      reads=["t"], writes=["t"])
                S.gp(lambda e: e.tensor_tensor(out=yg[:, :], in0=yg[:, :], in1=t_[:, :], op=ALU.mult),
                     reads=["t", "yg"], writes=["yg"])
                S.dve(lambda e, pr=pr: e.tensor_scalar(out=xc[:, :], in0=xrp[:, 0:TCH], scalar1=pr(0), scalar2=pr(4),
                                                       op0=ALU.mult, op1=ALU.add), reads=["xrp", "rgp"], writes=["xc"])
                for j in range(1, 4):
                    S.dve(lambda e, j=j, pr=pr: e.scalar_tensor_tensor(
                        out=xc[:, :], in0=xrp[:, j:j + TCH], scalar=pr(j), in1=xc[:, :],
                        op0=ALU.mult, op1=ALU.add), reads=["xrp", "rgp", "xc"], writes=["xc"])
                S.gp(lambda e: e.tensor_copy(out=xcb[:, :], in_=xc[:, :]), reads=["xc"], writes=["xcb"])
                for which, dst, dtok, bidx in ((0, r_, "r", 5), (1, i_, "i", 6)):
                    for sbk in range(TCH // 512):
                        cs = slice(sbk * 512, (sbk + 1) * 512)
                        ps, pstok = G.next()
                        S.pe(lambda e, ps=ps, cs=cs, which=which, cbi=cbi: e.matmul(
                            ps[:, :], lru_b[:, cbi, which, :], xcb[:, cs], start=True, stop=True),
                            reads=["lruw", "xcb"], writes=[pstok])
                        S.act(lambda e, ps=ps, cs=cs, dst=dst, bidx=bidx, pr=pr: e.activation(
                            out=dst[:, cs], in_=ps[:, :], func=AF.Sigmoid, bias=pr(bidx), scale=1.0),
                            reads=[pstok, "rgp"], writes=[dtok])
                S.act(lambda e, cbi=cbi: e.activation(out=a_[:, :], in_=r_[:, :], func=AF.Exp, scale=cch[:, cbi, 0:1]),
                      reads=["r", "cch"], writes=["a"])
                S.act(lambda e, cbi=cbi: e.activation(out=t_[:, :], in_=r_[:, :], func=AF.Exp, scale=cch[:, cbi, 1:2]),
                      reads=["r", "cch", "yg"], writes=["t"])
                S.act(lambda e: e.activation(out=t_[:, :], in_=t_[:, :], func=AF.Sqrt, bias=one[:, 0:1], scale=-1.0),
                      reads=["t", "one"], writes=["t"])
                S.dve(lambda e: e.tensor_tensor(out=t_[:, :], in0=t_[:, :], in1=i_[:, :], op=ALU.mult),
                      reads=["t", "i"], writes=["t"])
                S.dve(lambda e: e.tensor_tensor(out=t_[:, :], in0=t_[:, :], in1=xc[:, :], op=ALU.mult),
                      reads=["t", "xc"], writes=["t"])
                S.dve(lambda e: e.tensor_tensor_scan(out=r_[:, :], data0=a_[:, :], data1=t_[:, :],
                                                     initial=carry[:, 0:1], op0=ALU.mult, op1=ALU.add),
                      reads=["a", "t", "carry"], writes=["r"])
                S.dve(lambda e: e.tensor_copy(out=carry[:, :], in_=r_[:, TCH - 1:TCH]), reads=["r"], writes=["carry"])
                ot, otok = oab.next()
                S.gp(lambda e, ot=ot: e.tensor_tensor(out=ot[:, :], in0=r_[:, :], in1=yg[:, :], op=ALU.mult),
                     reads=["r", "yg"], writes=[otok])
                fins.append(S.dma(oaTv[cbi * 128:(cbi + 1) * 128, ch * TCH:(ch + 1) * TCH], ot[:, :], reads=[otok]))

    def bcast_mid(ap2d, n):
        a = ap2d.ap
        return bass.AP(ap2d.tensor, ap2d.offset, [list(a[0]), [0, n], list(a[1])])

    def bcast_last(ap2d, n):
        a = ap2d.ap
        return bass.AP(ap2d.tensor, ap2d.offset, [list(a[0]), list(a[1]), [0, n]])

    cos2 = S.sb([128, SEQ], F32, "cos2")
    sin2 = S.sb([128, SEQ], F32, "sin2")
    PI_LO = 3.1415925
    with S.scope():
        RCH = 2048
        pi_ = S.sb([128, RCH], I32, "posi")
        ang = S.sb([128, RCH], F32, "ang")
        kf = S.sb([128, RCH], F32, "kf")
        mm = S.sb([128, RCH], F32, "mm")
        ki = S.sb([128, RCH], I32, "ki")
        for ch in range(SEQ // RCH):
            cs = slice(ch * RCH, (ch + 1) * RCH)
            S.dma(pi_[:, :], pos[0:1, cs].partition_broadcast(128), writes=["posi"])
            S.dve(lambda e: e.tensor_scalar(out=ang[:, :], in0=pi_[:, :], scalar1=rc_t[:, 0:1], scalar2=None,
                                            op0=ALU.mult), reads=["posi", "rc"], writes=["ang"])
            S.dve(lambda e: e.tensor_scalar(out=ki[:, :], in0=ang[:, :], scalar1=float(1.0 / TWO_PI), scalar2=None,
                                            op0=ALU.mult), reads=["ang"], writes=["ki"])
            S.dve(lambda e: e.tensor_copy(out=kf[:, :], in_=ki[:, :]), reads=["ki"], writes=["kf"])
            for cw in (CW1, CW2, CW3):
                S.dve(lambda e, cw=cw: e.scalar_tensor_tensor(out=ang[:, :], in0=kf[:, :], scalar=-cw, in1=ang[:, :],
                                                              op0=ALU.mult, op1=ALU.add),
                      reads=["kf", "ang"], writes=["ang"])
            S.dve(lambda e: e.tensor_scalar(out=ang[:, :], in0=ang[:, :], scalar1=-PI_LO, scalar2=PI_LO,
                                            op0=ALU.max, op1=ALU.min), reads=["ang"], writes=["ang"])
            S.act(lambda e, cs=cs: e.activation(out=sin2[:, cs], in_=ang[:, :], func=AF.Sin, scale=rc_t[:, 1:2]),
                  reads=["ang", "rc"], writes=["sin2"])
            S.dve(lambda e: e.tensor_scalar(out=kf[:, :], in0=ang[:, :], scalar1=float(np.pi / 2), scalar2=None,
                                            op0=ALU.add), reads=["ang"], writes=["kf"])
            S.dve(lambda e: e.tensor_scalar(out=mm[:, :], in0=kf[:, :], scalar1=PI_LO, scalar2=float(-TWO_PI),
                                            op0=ALU.is_gt, op1=ALU.mult), reads=["kf"], writes=["mm"])
            S.dve(lambda e: e.tensor_tensor(out=kf[:, :], in0=kf[:, :], in1=mm[:, :], op=ALU.add),
                  reads=["kf", "mm"], writes=["kf"])
            S.dve(lambda e: e.tensor_scalar(out=kf[:, :], in0=kf[:, :], scalar1=-PI_LO, scalar2=PI_LO,
                                            op0=ALU.max, op1=ALU.min), reads=["kf"], writes=["kf"])
            S.act(lambda e, cs=cs: e.activation(out=cos2[:, cs], in_=kf[:, :], func=AF.Sin),
                  reads=["kf"], writes=["cos2"])

    T1 = Rot(S, 2, [128, 512], F32, "t1")
    T2 = Rot(S, 2, [128, 512], F32, "t2")

    def rope(ps1, ps1tok, ps2, ps2tok, np_, n, cos_ap, sin_ap, out_ap, out_toks, view=None):
        t1, t1tok = T1.next()
        t2, t2tok = T2.next()
        S.dve(lambda e: e.tensor_tensor(out=t1[0:np_, 0:n], in0=ps1[0:np_, 0:n], in1=cos_ap, op=ALU.mult),
              reads=[ps1tok, "cos2"], writes=[t1tok])
        S.dve(lambda e: e.tensor_tensor(out=t2[0:np_, 0:n], in0=ps2[0:np_, 0:n], in1=sin_ap, op=ALU.mult),
              reads=[ps2tok, "sin2"], writes=[t2tok])
        a1 = t1[0:np_, 0:n]
        a2 = t2[0:np_, 0:n]
        if view is not None:
            a1 = view(a1)
            a2 = view(a2)
        S.gp(lambda e: e.tensor_tensor(out=out_ap, in0=a1, in1=a2, op=ALU.add),
             reads=[t1tok, t2tok], writes=out_toks)

    kcmpT = S.sb([64, 512], BF16, "kcmpT")
    vaug_c = S.sb([128, 4, 193], BF16, "vaugc")
    S.dve(lambda e: e.memset(kcmpT[:, :], 0.0), writes=["kcmpT"])
    S.dve(lambda e: e.memset(vaug_c[:, :, 64:65], 1.0), writes=["vaugc"])
    S.dve(lambda e: e.tensor_copy(out=vaug_c[:, :, 65:193], in_=ovl_t[:, :, :]), reads=["ovl"], writes=["vaugc"])
    if stop_after >= 1:
      with S.scope():
        kcT = S.sb([64, SEQ], BF16, "kcT")
        vcT = S.sb([64, SEQ], BF16, "vcT")
        w1b = S.sb([64, 32, 256], BF16, "w1b")
        st1 = Rot(S, 2, [64, 8, 256], F32, "st1")
        hid = S.sb([128, 2, 512], BF16, "hid")
        xh = S.sb([128, 512], F32, "xh")
        th = S.sb([128, 512], F32, "th")
        bvec = S.sb([128, 2], F32, "bvec")
        S.dve(lambda e: e.memset(hid[:, :, :], 0.0), writes=["hid"])
        for tb in range(SEQ // 512):
            hb, htok = load_h(tb)
            blk = slice(tb * 512, (tb + 1) * 512)
            for (c0, dst, dtok) in ((0, kcT, "kcT"), (64, vcT, "vcT")):
                ps, pstok = G.next()
                for c in range(8):
                    S.pe(lambda e, c=c, ps=ps, hb=hb, c0=c0: e.matmul(
                        ps[0:64, :], wk_b[:, c, c0:c0 + 64], hb[:, c, :], start=(c == 0), stop=(c == 7)),
                        reads=["wk", htok], writes=[pstok])
                S.act(lambda e, ps=ps, dst=dst, blk=blk: e.activation(out=dst[:, blk], in_=ps[0:64, :], func=AF.Copy),
                      reads=[pstok], writes=[dtok])
        for j in range(2):
            src, stok_ = (kcT, "kcT") if j == 0 else (vcT, "vcT")
            for pc in range(4):
                s_, s_tok = st1.next()
                S.dma(s_[:, :, :], cw1[j, :, pc * 8:(pc + 1) * 8, :], writes=[s_tok])
                S.gp(lambda e, s_=s_, pc=pc: e.tensor_copy(out=w1b[:, pc * 8:(pc + 1) * 8, :], in_=s_[:, :, :]),
                     reads=[s_tok], writes=["w1b"])
            for half in range(2):
                hs = slice(half * 128, (half + 1) * 128)
                ps, pstok = G.next()
                for p in range(32):
                    S.pe(lambda e, p=p, ps=ps, src=src, hs=hs: e.matmul(
                        ps[:, 0:511], w1b[:, p, hs], src[:, p:p + 8161:16], start=(p == 0), stop=(p == 31)),
                        reads=["w1b", stok_], writes=[pstok])
                psb, psbtok = G.next()
                for p in range(32):
                    S.pe(lambda e, p=p, psb=psb, hs=hs, j=j: e.matmul(
                        psb[:, 0:1], w1b[:, p, hs], cpe_b[:, j, p:p + 1], start=(p == 0), stop=(p == 31)),
                        reads=["w1b", "cpe"], writes=[psbtok])
                S.dve(lambda e, psb=psb, half=half, j=j: e.tensor_tensor(
                    out=bvec[:, half:half + 1], in0=psb[:, 0:1], in1=cb1_t[:, j, half:half + 1], op=ALU.add),
                    reads=[psbtok, "cb1"], writes=["bvec"])
                S.act(lambda e, ps=ps, half=half: e.activation(out=xh[:, 0:511], in_=ps[:, 0:511], func=AF.Identity,
                                                               bias=bvec[:, half:half + 1], scale=1.0),
                      reads=[pstok, "bvec"], writes=["xh"])
                S.act(lambda e: e.activation(out=th[:, 0:511], in_=xh[:, 0:511], func=AF.Square),
                      reads=["xh"], writes=["th"])
                S.dve(lambda e: e.tensor_scalar(out=th[:, 0:511], in0=th[:, 0:511], scalar1=0.044715, scalar2=1.0,
                                                op0=ALU.mult, op1=ALU.add), reads=["th"], writes=["th"])
                S.dve(lambda e: e.tensor_tensor(out=th[:, 0:511], in0=th[:, 0:511], in1=xh[:, 0:511], op=ALU.mult),
                      reads=["th", "xh"], writes=["th"])
                S.act(lambda e: e.activation(out=th[:, 0:511], in_=th[:, 0:511], func=AF.Sigmoid, scale=GELU_C),
                      reads=["th"], writes=["th"])
                S.gp(lambda e, half=half: e.tensor_tensor(out=hid[:, half, 0:511], in0=xh[:, 0:511], in1=th[:, 0:511],
                                                          op=ALU.mult), reads=["th", "xh"], writes=["hid"])
            if j == 0:
                ps1, ps1tok = G.next()
                ps2, ps2tok = G.next()
                for (pp, pptok, c0) in ((ps1, ps1tok, 0), (ps2, ps2tok, 64)):
                    for half in range(2):
                        S.pe(lambda e, pp=pp, half=half, c0=c0: e.matmul(
                            pp[0:64, :], cw2_b[:, half, c0:c0 + 64], hid[:, half, :],
                            start=(half == 0), stop=(half == 1)), reads=["cw2", "hid"], writes=[pptok])
                rope(ps1, ps1tok, ps2, ps2tok, 64, 511, cos2[0:64, 31:SEQ:16], sin2[0:64, 31:SEQ:16],
                     kcmpT[:, 0:511], ["kcmpT"])
            else:
                for ct in range(4):
                    ps, pstok = G.next()
                    for half in range(2):
                        S.pe(lambda e, ps=ps, half=half, ct=ct: e.matmul(
                            ps[:, 0:64], hid[:, half, ct * 128:(ct + 1) * 128], cw2_b[:, half, 128:192],
                            start=(half == 0), stop=(half == 1)), reads=["cw2", "hid"], writes=[pstok])
                    S.act(lambda e, ps=ps, ct=ct: e.activation(out=vaug_c[:, ct, 0:64], in_=ps[:, 0:64], func=AF.Copy),
                          reads=[pstok], writes=["vaugc"])

    kT = S.sb([128, SEQ], BF16, "kT")
    vs_aug = S.sb([128, NQT, 65], BF16, "vs")
    vw_aug = S.sb([128, NQT, 65], BF16, "vw")
    gs = S.sb([128, NQT, 12], F32, "gs")
    S.dve(lambda e: e.memset(vs_aug[:, :, 64:65], 1.0), writes=["vs"])
    S.dve(lambda e: e.memset(vw_aug[:, :, 64:65], 1.0), writes=["vw"])
    if stop_after >= 2:
        for tb in range(SEQ // 512):
            hb, htok = load_h(tb)
            blk = slice(tb * 512, (tb + 1) * 512)
            ps1, ps1tok = G.next()
            ps2, ps2tok = G.next()
            for (pp, pptok, c0) in ((ps1, ps1tok, 128), (ps2, ps2tok, 256)):
                for c in range(8):
                    S.pe(lambda e, c=c, pp=pp, hb=hb, c0=c0: e.matmul(
                        pp[:, :], wk_b[:, c, c0:c0 + 128], hb[:, c, :], start=(c == 0), stop=(c == 7)),
                        reads=["wk", htok], writes=[pptok])
            rope(ps1, ps1tok, ps2, ps2tok, 128, 512, cos2[:, blk], sin2[:, blk], kT[:, blk], ["kT"])
            for tt in range(4):
                tile_ = tb * 4 + tt
                ts_ = slice(tt * 128, (tt + 1) * 128)
                ps, pstok = G.next()
                for c in range(8):
                    S.pe(lambda e, c=c, ps=ps, hb=hb, ts_=ts_: e.matmul(
                        ps[:, 0:128], hb[:, c, ts_], wk_b[:, c, 384:512], start=(c == 0), stop=(c == 7)),
                        reads=["wk", htok], writes=[pstok])
                S.act(lambda e, ps=ps, tile_=tile_: e.activation(out=vs_aug[:, tile_, 0:64], in_=ps[:, 0:64], func=AF.Copy),
                      reads=[pstok], writes=["vs"])
                S.act(lambda e, ps=ps, tile_=tile_: e.activation(out=vw_aug[:, tile_, 0:64], in_=ps[:, 64:128], func=AF.Copy),
                      reads=[pstok], writes=["vw"])
                psg, psgtok = G.next()
                for c in range(8):
                    S.pe(lambda e, c=c, psg=psg, hb=hb, ts_=ts_: e.matmul(
                        psg[:, 0:12], hb[:, c, ts_], wg_b[:, c, :], start=(c == 0), stop=(c == 7)),
                        reads=["wg", htok], writes=[psgtok])
                S.act(lambda e, psg=psg, tile_=tile_: e.activation(out=gs[:, tile_, :], in_=psg[:, 0:12], func=AF.Sigmoid),
                      reads=[psgtok], writes=["gs"])

    if debug:
        dbg = {
            "d_cos": (cos2, [128, SEQ], F32, ["cos2"]), "d_sin": (sin2, [128, SEQ], F32, ["sin2"]),
            "d_kT": (kT, [128, SEQ], BF16, ["kT"]), "d_kcmpT": (kcmpT, [64, 512], BF16, ["kcmpT"]),
            "d_vaugc": (vaug_c, [128, 4, 193], BF16, ["vaugc"]), "d_gs": (gs, [128, NQT, 12], F32, ["gs"]),
            "d_vs": (vs_aug, [128, NQT, 65], BF16, ["vs"]),
        }
        for nm, (dbuf, shp, dt_, toks) in dbg.items():
            o_ = _dram_out(nc, nm, shp, dt_)
            if len(shp) == 2:
                fins.append(S.dma(o_, dbuf[:, :], reads=toks))
            else:
                fins.append(S.dma(o_, dbuf[:, :, :], reads=toks))
    if stop_after >= 3:
        QT = Rot(S, 2, [128, 4, 4, 128], BF16, "qT")
        selx = S.sb([128, SEQ], BF16, "selx")
        EB = Rot(S, 3, [128, 512], BF16, "E")
        EC = Rot(S, 2, [128, 512], F32, "Ec")
        PB = Rot(S, 3, [128, 4, 128], BF16, "P")
        MB = Rot(S, 3, [128, 128], BF16, "Mb")
        PM = S.ps([128, 4, 128], F32, "pm")
        OC = [S.ps([128, 2, 256], F32, "oc") for _ in range(2)]
        OS = S.ps([128, 4, 128], F32, "os")
        OW = S.ps([128, 4, 128], F32, "ow")
        rsc = S.sb([128, 4], F32, "rsc")
        rss = S.sb([128, 4], F32, "rss")
        rsw = S.sb([128, 4], F32, "rsw")
        impb = S.sb([128, 128], F32, "impb")
        tmpi = S.sb([128, 128], F32, "tmpi")
        m1 = S.sb([128, 8], F32, "m1")
        m2 = S.sb([128, 8], F32, "m2")
        sel = S.sb([128, 128], BF16, "sel")
        o32 = S.sb([128, 4, 64], F32, "o32")
        OBr = Rot(S, 2, [128, 4, 64], BF16, "obt")
        pmi = [0]
        gsv = gs[:, :, :].rearrange("p t (h j) -> p t h j", j=3)
        v3 = lambda a: a.rearrange("p (j q) -> p j q", j=4)
        for tb in range(SEQ // 512):
            hb, htok = load_h(tb)
            blk = slice(tb * 512, (tb + 1) * 512)
            qt_, qtok = QT.next()
            for h in range(4):
                ps1, ps1tok = G.next()
                ps2, ps2tok = G.next()
                for (pp, pptok, wsel) in ((ps1, ps1tok, 0), (ps2, ps2tok, 1)):
                    for c in range(8):
                        S.pe(lambda e, c=c, pp=pp, hb=hb, wsel=wsel, h=h: e.matmul(
                            pp[:, :], wq_b[wsel][:, c, h * 128:(h + 1) * 128], hb[:, c, :],
                            start=(c == 0), stop=(c == 7)), reads=[f"wq{wsel}", htok], writes=[pptok])
                rope(ps1, ps1tok, ps2, ps2tok, 128, 512, cos2[:, blk], sin2[:, blk],
                     qt_[:, :, h, :], [qtok], view=v3)
            for j in range(4):
                qt = tb * 4 + j
                q0 = qt_[0:64, j, :, :].rearrange("p h q -> p (h q)")
                q1 = qt_[64:128, j, :, :].rearrange("p h q -> p (h q)")
                nct = (8 * qt + 7 + 127) // 128
                for ct in range(nct):
                    ps, pstok = G.next()
                    S.pe(lambda e, ps=ps, ct=ct, q0=q0: e.matmul(
                        ps[:, :], kcmpT[:, ct * 128:(ct + 1) * 128], q0, start=True, stop=True),
                        reads=["kcmpT", qtok], writes=[pstok])
                    ec, ectok = EC.next()
                    S.act(lambda e, ps=ps, ec=ec: e.activation(out=ec[:, :], in_=ps[:, :], func=AF.Exp, scale=SCALE),
                          reads=[pstok], writes=[ectok])
                    pb, pbtok = PB.next()
                    dm = qt - 16 * ct
                    if dm >= 17:
                        S.dve(lambda e, pb=pb, ec=ec: e.tensor_copy(out=pb[:, :, :], in_=v3(ec[:, :])),
                              reads=[ectok], writes=[pbtok])
                    else:
                        S.dve(lambda e, pb=pb, ec=ec, dm=dm: e.tensor_tensor(
                            out=pb[:, :, :], in0=v3(ec[:, :]), in1=bcast_mid(cmask_t[:, dm, :], 4), op=ALU.mult),
                            reads=[ectok, "cmask"], writes=[pbtok])
                    for h in range(4):
                        S.pe(lambda e, pb=pb, h=h, ct=ct, nct=nct: e.matmul(
                            OC[h // 2][:, h % 2, 0:193], pb[:, h, :], vaug_c[:, ct, :],
                            start=(ct == 0 and h % 2 == 0), stop=(ct == nct - 1), skip_group_check=True),
                            reads=[pbtok, "vaugc"], writes=[("oc", h // 2)])
                for bnk in range(2):
                    S.dve(lambda e, bnk=bnk: e.tensor_scalar(
                        out=rsc[:, 2 * bnk:2 * bnk + 2], in0=OC[bnk][:, :, 64], scalar1=1e-30, scalar2=None,
                        op0=ALU.max), reads=[("oc", bnk)], writes=["rsc"])
                S.dve(lambda e: e.reciprocal(out=rsc[:, :], in_=rsc[:, :]), reads=["rsc"], writes=["rsc"])
                for h in range(4):
                    in1 = bw_t[:, 128 - 2 * qt:256 - 2 * qt] if h == 0 else impb[:, :]
                    S.dve(lambda e, h=h, in1=in1: e.scalar_tensor_tensor(
                        out=impb[:, :], in0=OC[h // 2][:, h % 2, 65:193], scalar=rsc[:, h:h + 1], in1=in1,
                        op0=ALU.mult, op1=ALU.add), reads=[("oc", h // 2), "rsc", "impb", "bw"], writes=["impb"])
                S.dve(lambda e: e.tensor_scalar(out=impb[:, 0:1], in0=impb[:, 0:1], scalar1=1e4, scalar2=None,
                                                op0=ALU.add), reads=["impb"], writes=["impb"])
                S.dve(lambda e: e.max(out=m1[:, :], in_=impb[:, :]), reads=["impb"], writes=["m1"])
                S.dve(lambda e: e.match_replace(out=tmpi[:, :], in_to_replace=m1[:, :], in_values=impb[:, :],
                                                imm_value=-3e4), reads=["impb", "m1"], writes=["tmpi"])
                S.dve(lambda e: e.max(out=m2[:, :], in_=tmpi[:, :]), reads=["tmpi"], writes=["m2"])
                S.dve(lambda e: e.tensor_scalar(out=sel[:, :], in0=impb[:, :], scalar1=m2[:, 7:8], scalar2=1.0,
                                                op0=ALU.is_ge, op1=ALU.mult), reads=["impb", "m2"], writes=["sel"])
                nb = 2 * (qt + 1)
                S.dve(lambda e, nb=nb: e.tensor_copy(
                    out=selx[:, 0:nb * 64].rearrange("p (a b) -> p a b", b=64), in_=bcast_last(sel[:, 0:nb], 64)),
                    reads=["sel"], writes=["selx"])
                for kt in range(qt + 1):
                    ks_ = slice(kt * 128, (kt + 1) * 128)
                    ps, pstok = G.next()
                    S.pe(lambda e, ps=ps, ks_=ks_, q0=q0: e.matmul(ps[:, :], kT[0:64, ks_], q0, start=True, stop=True),
                         reads=["kT", qtok], writes=[pstok])
                    pmi[0] = (pmi[0] + 1) % 4
                    slot = pmi[0]
                    S.pe(lambda e, ks_=ks_, slot=slot: e.matmul(PM[:, slot, :], selx[:, ks_], ident_t[:, :],
                                                                start=True, stop=True),
                         reads=["selx", "ident"], writes=[("pm", slot)])
                    eb, ebtok = EB.next()
                    S.act(lambda e, ps=ps, eb=eb: e.activation(out=eb[:, :], in_=ps[:, :], func=AF.Exp, scale=SCALE),
                          reads=[pstok], writes=[ebtok])
                    mb, mbtok = MB.next()
                    if kt == qt:
                        S.dve(lambda e, mb=mb, slot=slot: e.tensor_tensor(
                            out=mb[:, :], in0=PM[:, slot, :], in1=tri_t[:, 0, :], op=ALU.mult),
                            reads=[("pm", slot), "tri"], writes=[mbtok])
                    else:
                        S.act(lambda e, mb=mb, slot=slot: e.activation(out=mb[:, :], in_=PM[:, slot, :], func=AF.Copy),
                              reads=[("pm", slot)], writes=[mbtok])
                    pb, pbtok = PB.next()
                    S.dve(lambda e, pb=pb, eb=eb, mb=mb: e.tensor_tensor(
                        out=pb[:, :, :], in0=v3(eb[:, :]), in1=bcast_mid(mb[:, :], 4), op=ALU.mult),
                        reads=[ebtok, mbtok], writes=[pbtok])
                    for h in range(4):
                        S.pe(lambda e, pb=pb, h=h, kt=kt, qt=qt: e.matmul(
                            OS[:, h, 0:65], pb[:, h, :], vs_aug[:, kt, :], start=(kt == 0 and h == 0),
                            stop=(kt == qt), skip_group_check=True),
                            reads=[pbtok, "vs"], writes=["os"])
                k0 = max(0, qt - 4)
                for kt in range(k0, qt + 1):
                    ks_ = slice(kt * 128, (kt + 1) * 128)
                    ps, pstok = G.next()
                    S.pe(lambda e, ps=ps, ks_=ks_, q1=q1: e.matmul(ps[:, :], kT[64:128, ks_], q1, start=True, stop=True),
                         reads=["kT", qtok], writes=[pstok])
                    pb, pbtok = PB.next()
                    if kt == qt or kt == qt - 4:
                        eb, ebtok = EB.next()
                        S.act(lambda e, ps=ps, eb=eb: e.activation(out=eb[:, :], in_=ps[:, :], func=AF.Exp, scale=SCALE),
                              reads=[pstok], writes=[ebtok])
                        ti = 0 if kt == qt else 1
                        S.dve(lambda e, pb=pb, eb=eb, ti=ti: e.tensor_tensor(
                            out=pb[:, :, :], in0=v3(eb[:, :]), in1=bcast_mid(tri_t[:, ti, :], 4), op=ALU.mult),
                            reads=[ebtok, "tri"], writes=[pbtok])
                    else:
                        S.act(lambda e, ps=ps, pb=pb: e.activation(
                            out=pb[:, :, :].rearrange("p h q -> p (h q)"), in_=ps[:, :], func=AF.Exp, scale=SCALE),
                            reads=[pstok], writes=[pbtok])
                    for h in range(4):
                        S.pe(lambda e, pb=pb, h=h, kt=kt, qt=qt, k0=k0: e.matmul(
                            OW[:, h, 0:65], pb[:, h, :], vw_aug[:, kt, :], start=(kt == k0 and h == 0),
                            stop=(kt == qt), skip_group_check=True),
                            reads=[pbtok, "vw"], writes=["ow"])
                if debug and qt in (3, 20):
                    if "dq" not in K:
                        K["dq"] = S.sb([128, 512], F32, "dq")
                    dq = K["dq"]
                    o_ = _dram_out(nc, f"d_q{qt}", [128, 2304], F32)
                    pieces = [(OC[0][:, :, :].rearrange("p a b -> p (a b)"), 512, [("oc", 0)]),
                              (OC[1][:, :, :].rearrange("p a b -> p (a b)"), 512, [("oc", 1)]),
                              (OS[:, :, :].rearrange("p a b -> p (a b)"), 512, ["os"]),
                              (OW[:, :, :].rearrange("p a b -> p (a b)"), 512, ["ow"]),
                              (impb[:, :], 128, ["impb"]), (sel[:, :], 128, ["sel"])]
                    off = 0
                    for (src_ap, n_, tk) in pieces:
                        S.act(lambda e, src_ap=src_ap, n_=n_: e.activation(out=dq[:, 0:n_], in_=src_ap, func=AF.Copy),
                              reads=tk, writes=["dq"])
                        fins.append(S.dma(o_[:, off:off + n_], dq[:, 0:n_], reads=["dq"]))
                        off += n_
                for (rs_, rtok, src, stok_, gi) in ((rss, "rss", OS, "os", 1), (rsw, "rsw", OW, "ow", 2)):
                    S.dve(lambda e, rs_=rs_, src=src: e.tensor_scalar(out=rs_[:, :], in0=src[:, :, 64], scalar1=1e-30,
                                                                      scalar2=None, op0=ALU.max),
                          reads=[stok_], writes=[rtok])
                    S.dve(lambda e, rs_=rs_: e.reciprocal(out=rs_[:, :], in_=rs_[:, :]), reads=[rtok], writes=[rtok])
                    S.dve(lambda e, rs_=rs_, gi=gi, qt=qt: e.tensor_tensor(out=rs_[:, :], in0=rs_[:, :],
                                                                          in1=gsv[:, qt, :, gi], op=ALU.mult),
                          reads=[rtok, "gs"], writes=[rtok])
                S.dve(lambda e, qt=qt: e.tensor_tensor(out=rsc[:, :], in0=rsc[:, :], in1=gsv[:, qt, :, 0], op=ALU.mult),
                      reads=["rsc", "gs"], writes=["rsc"])
                obt, obtok = OBr.next()
                for h in range(4):
                    S.dve(lambda e, h=h: e.tensor_scalar(out=o32[:, h, :], in0=OC[h // 2][:, h % 2, 0:64],
                                                         scalar1=rsc[:, h:h + 1], scalar2=None, op0=ALU.mult),
                          reads=[("oc", h // 2), "rsc"], writes=["o32"])
                    S.dve(lambda e, h=h: e.scalar_tensor_tensor(out=o32[:, h, :], in0=OS[:, h, 0:64],
                                                                scalar=rss[:, h:h + 1], in1=o32[:, h, :],
                                                                op0=ALU.mult, op1=ALU.add),
                          reads=["os", "rss", "o32"], writes=["o32"])
                    S.dve(lambda e, h=h, obt=obt: e.scalar_tensor_tensor(out=obt[:, h, :], in0=OW[:, h, 0:64],
                                                                         scalar=rsw[:, h:h + 1], in1=o32[:, h, :],
                                                                         op0=ALU.mult, op1=ALU.add),
                          reads=["ow", "rsw", "o32"], writes=[obtok])
                fins.append(S.dma(ob[qt * 128:(qt + 1) * 128, :], obt[:, :, :].rearrange("p h d -> p (h d)"),
                                  reads=[obtok], q="pool"))
    S.finish(fins)
    return nc


_CONST = {}


def _consts():
    if _CONST:
        return _CONST
    half = 32
    inv = (10000.0 ** (-np.arange(half, dtype=np.float32) * np.float32(2.0) / np.float32(64))).astype(np.float32)
    rc = np.zeros((128, 4), np.float32)
    p = np.arange(128)
    rc[:, 0] = inv[p % 32]
    rc[:, 1] = np.where((p % 64) < 32, -1.0, 1.0)
    _CONST["rc"] = rc
    c0 = np.arange(512)[:, None] * 16
    s0 = np.arange(128)[None, :] * 64
    ov = np.clip(np.minimum(c0 + 32, s0 + 64) - np.maximum(c0, s0), 0, None).astype(np.float32) / 32.0
    ov[511] = 0.0
    _CONST["ovl"] = np.ascontiguousarray(ov.reshape(4, 128, 128).transpose(1, 0, 2)).astype(NPBF)
    cl = np.arange(128)[:, None, None]
    dm = np.arange(17)[None, :, None]
    q = np.arange(128)[None, None, :]
    _CONST["cmask"] = ((16 * cl + 31 - q) <= 128 * dm).astype(np.float32).astype(NPBF)
    kl = np.arange(128)[:, None]
    ql = np.arange(128)[None, :]
    tri = np.stack([(kl <= ql), (kl > ql)], axis=1).astype(np.float32)
    _CONST["tri"] = tri.astype(NPBF)
    _CONST["ident"] = np.eye(128, dtype=np.float32).astype(NPBF)
    bwm = np.zeros((128, 256), np.float32)
    for qlv in range(128):
        cur = 1 if qlv >= 64 else 0
        for idx in range(256):
            jp = idx - 128
            if jp == cur:
                bwm[qlv, idx] = 2e4
            elif jp > cur:
                bwm[qlv, idx] = -1e4 - jp
    _CONST["bw"] = bwm
    return _CONST


def _swap_half(cols):
    return np.concatenate([cols[32:], cols[:32]])


def b_inputs(inp, l, core, hT_b):
    b, g = divmod(core, 4)
    C = _consts()
    w_in = inp["w_in"][l]
    cbs = (2 * g, 2 * g + 1)
    cols = []
    for cb in cbs:
        cols.append(np.arange(OFF_XR + cb * 128, OFF_XR + (cb + 1) * 128))
    for cb in cbs:
        cols.append(np.arange(OFF_YR + cb * 128, OFF_YR + (cb + 1) * 128))
    w_rg = np.ascontiguousarray(w_in[:, np.concatenate(cols)])
    rgp = np.zeros((128, 2, 16), np.float32)
    lruw = np.zeros((128, 2, 2, 128), np.float32)
    for i, cb in enumerate(cbs):
        ch = slice(cb * 128, (cb + 1) * 128)
        rgp[:, i, 0:4] = inp["conv_w"][l][:, ch].T
        rgp[:, i, 4] = inp["conv_b"][l][ch]
        rgp[:, i, 5] = inp["lru_br"][l][cb]
        rgp[:, i, 6] = inp["lru_bi"][l][cb]
        rgp[:, i, 7] = inp["lru_lambda"][l][ch]
        lruw[:, i, 0, :] = inp["lru_wr"][l][cb]
        lruw[:, i, 1, :] = inp["lru_wi"][l][cb]
    qc, qsc = [], []
    for h in range(4):
        base = np.arange(OFF_Q + (4 * g + h) * 64, OFF_Q + (4 * g + h + 1) * 64)
        qc += [base, base]
        qsc += [_swap_half(base), _swap_half(base)]
    w_q = np.ascontiguousarray(w_in[:, np.concatenate(qc + qsc)])
    kvb = lambda j: np.arange(OFF_KV + j * 256 + g * 64, OFF_KV + j * 256 + (g + 1) * 64)
    kc, vc, ks, vs, kw, vw = [kvb(j) for j in range(6)]
    w_k = np.ascontiguousarray(w_in[:, np.concatenate([kc, vc, ks, kw, _swap_half(ks), _swap_half(kw), vs, vw])])
    w_g = np.ascontiguousarray(w_in[:, OFF_G + 12 * g: OFF_G + 12 * (g + 1)])
    cw1 = np.ascontiguousarray(inp["cmp_w1"][l].reshape(2, 32, 64, 256).transpose(0, 2, 1, 3))
    cpe = np.ascontiguousarray(inp["cmp_pe"][l].transpose(2, 0, 1))
    cb1 = np.ascontiguousarray(inp["cmp_b1"][l].reshape(2, 2, 128).transpose(2, 0, 1))
    w2k = inp["cmp_w2"][l][0]
    w2v = inp["cmp_w2"][l][1]
    sw = _swap_half(np.arange(64))
    cw2 = np.concatenate([w2k, w2k[:, sw], w2v], axis=1).reshape(2, 128, 192).transpose(1, 0, 2)
    return {
        "hT": hT_b, "pos": np.ascontiguousarray(inp["positions"][b:b + 1]),
        "w_rg": w_rg, "rgp": rgp, "lruw": lruw, "w_q": w_q, "w_k": w_k, "w_g": w_g,
        "rc": C["rc"], "cw1": cw1, "cpe": cpe, "cb1": cb1, "cw2": np.ascontiguousarray(cw2),
        "ovl": C["ovl"], "cmask": C["cmask"], "tri": C["tri"], "ident": C["ident"], "bw": C["bw"],
    }


def _gl(v):
    return np.ascontiguousarray(np.asarray(v, np.float32).reshape(8, 128).T)


def c_common(inp, l):
    w_in = inp["w_in"][l]
    nxt = inp["ln_mix_pre"][min(l + 1, DEPTH - 1)]
    gains = np.concatenate([_gl(inp["ln_mix_pre"][l]), _gl(inp["ln_mem"][l]), _gl(inp["ln_mix_post"][l]),
                            _gl(inp["ln_mlp_pre"][l]), _gl(inp["ln_mlp_post"][l]), _gl(nxt)], axis=1)
    return {
        "gains": np.ascontiguousarray(gains),
        "w_qm": np.ascontiguousarray(w_in[:, OFF_QM:OFF_QM + 1024]),
        "w_gm": np.ascontiguousarray(w_in[:, OFF_GM:OFF_GM + 3072]),
        "w_kv": np.ascontiguousarray(inp["w_mem_kv"][l]),
        "w_ra": np.ascontiguousarray(inp["w_br_rnn"][l]),
        "w_rb": np.ascontiguousarray(inp["w_br_nsa"][l]),
        "w_rc": np.ascontiguousarray(inp["w_br_mem"][l]),
        "w_out": np.ascontiguousarray(inp["w_out"][l]),
        "w1": np.ascontiguousarray(inp["mlp_w1"][l]),
        "w2": np.ascontiguousarray(inp["mlp_w2"][l]),
    }


_PROGS = {}


def _prog(name):
    if name not in _PROGS:
        if name == "A":
            _PROGS[name] = build_A(SEQ // 4)
        elif name == "B":
            _PROGS[name] = build_B()
        else:
            _PROGS[name] = build_C(SEQ // 4)
    return _PROGS[name]


def kernel(**inputs):
    inp = {k: np.asarray(v) for k, v in inputs.items()}
    T = SEQ // 4
    cores = list(range(NCORE))
    x = inp["x"].astype(np.float32, copy=False)
    xT = [np.ascontiguousarray(x[c // 4, (c % 4) * T:(c % 4 + 1) * T].T) for c in cores]
    memT = [np.ascontiguousarray(inp["mem"][b].T.astype(np.float32)) for b in range(BATCH)]
    g0 = _gl(inp["ln_mix_pre"][0])
    res = run_bass_kernel_spmd(_prog("A"), [{"xT": xT[c], "g": g0} for c in cores], core_ids=cores)
    hT = [np.asarray(res.results[c]["hT"]) for c in cores]
    for l in range(DEPTH):
        hTb = [np.ascontiguousarray(np.concatenate(hT[4 * b:4 * b + 4], axis=1)) for b in range(BATCH)]
        res = run_bass_kernel_spmd(_prog("B"), [b_inputs(inp, l, c, hTb[c // 4]) for c in cores], core_ids=cores)
        oaT = [np.concatenate([np.asarray(res.results[4 * b + g]["oaT"]) for g in range(4)], axis=0)
               for b in range(BATCH)]
        obT = [np.concatenate([np.asarray(res.results[4 * b + g]["ob"]).T for g in range(4)], axis=0)
               for b in range(BATCH)]
        cc = c_common(inp, l)
        maps = []
        for c in cores:
            b, j = divmod(c, 4)
            m = dict(cc)
            m["xT"] = xT[c]
            m["oaT"] = np.ascontiguousarray(oaT[b][:, j * T:(j + 1) * T])
            m["obT"] = np.ascontiguousarray(obT[b][:, j * T:(j + 1) * T])
            m["memT"] = memT[b]
            maps.append(m)
        res = run_bass_kernel_spmd(_prog("C"), maps, core_ids=cores)
        xT = [np.asarray(res.results[c]["xoT"]) for c in cores]
        hT = [np.asarray(res.results[c]["hnT"]) for c in cores]
    out = np.empty((BATCH, SEQ, D), np.float32)
    for c in cores:
        b, j = divmod(c, 4)
        out[b, j * T:(j + 1) * T] = xT[c].T
    return out
```
